# Optimizing a Trainium2 kernel written in Bass

```python
import math
import jax, jax.numpy as jnp
from jax import lax
import numpy as np

D_MODEL = 2048
BATCH = 2
SEQ = 16384
DEPTH = 1

N_HEADS = 8
HEAD_DIM = 64
V_DIM = 2 * HEAD_DIM
ATTN_WIDTH = N_HEADS * V_DIM
Q_BLOCK = 128
CONV_CH = 1024
CONV_K = 31
D_FF = 5632
FFN_CONV_K = 3
PLE_DIM = 256
EPS = 1e-6
LN_EPS = 1e-5

Q_COLS = N_HEADS * 2 * HEAD_DIM
K_COLS = N_HEADS * 2 * HEAD_DIM
V_COLS = N_HEADS * V_DIM
GLU_COLS = 2 * CONV_CH
GATE_COLS = 2 * D_MODEL
IN_COLS = Q_COLS + K_COLS + V_COLS + GLU_COLS + GATE_COLS

kernel_name = "hybrid_diffattn_conformer_convffn_block"


def _rmsnorm(x, g):
    xf = x.astype(jnp.float32)
    y = xf * lax.rsqrt(jnp.mean(xf * xf, axis=-1, keepdims=True) + EPS)
    return (y * g.astype(jnp.float32)).astype(x.dtype)


def _layernorm(x, g, b):
    xf = x.astype(jnp.float32)
    mu = jnp.mean(xf, axis=-1, keepdims=True)
    var = jnp.mean(jnp.square(xf - mu), axis=-1, keepdims=True)
    y = (xf - mu) * lax.rsqrt(var + LN_EPS)
    return (y * g.astype(jnp.float32) + b.astype(jnp.float32)).astype(x.dtype)


def _causal_dwconv(x, w, b):
    k = w.shape[0]
    c = x.shape[-1]
    y = lax.conv_general_dilated(
        x, w[:, None, :].astype(x.dtype), window_strides=(1,), padding=[(k - 1, 0)],
        dimension_numbers=("NWC", "WIO", "NWC"), feature_group_count=c)
    return y + b.astype(x.dtype)


def _alibi_slopes():
    return jnp.exp2(-8.0 * jnp.arange(1, N_HEADS + 1, dtype=jnp.float32) / N_HEADS)


def _diff_attention(hq, hk, hv, lam, g_subln, lam_init):
    b, s = hq.shape[0], hq.shape[1]
    q = hq.reshape(b, s, N_HEADS, 2, HEAD_DIM).transpose(3, 0, 2, 1, 4)
    k = hk.reshape(b, s, N_HEADS, 2, HEAD_DIM).transpose(3, 0, 2, 1, 4)
    v = hv.reshape(b, s, N_HEADS, V_DIM).transpose(0, 2, 1, 3)
    n_blk = s // Q_BLOCK
    qb = q.reshape(2, b, N_HEADS, n_blk, Q_BLOCK, HEAD_DIM).transpose(3, 0, 1, 2, 4, 5)
    slopes = _alibi_slopes()
    k_pos = jnp.arange(s, dtype=jnp.int32)
    scale = HEAD_DIM ** -0.5

    def block(args):
        qi, i = args
        q_pos = i * Q_BLOCK + jnp.arange(Q_BLOCK, dtype=jnp.int32)
        dist = q_pos[:, None] - k_pos[None, :]
        bias = -slopes[:, None, None] * dist.astype(jnp.float32)[None]
        sc = jnp.einsum("nbhqd,nbhkd->nbhqk", qi, k).astype(jnp.float32) * scale + bias
        sc = jnp.where(dist >= 0, sc, -jnp.inf)
        pr = jax.nn.softmax(sc, axis=-1)
        a = pr[0] - lam * pr[1]
        return jnp.einsum("bhqk,bhke->bhqe", a.astype(v.dtype), v)

    out = lax.map(block, (qb, jnp.arange(n_blk, dtype=jnp.int32)))
    out = out.transpose(1, 0, 3, 2, 4).reshape(b, s, N_HEADS, V_DIM)
    out = _rmsnorm(out, g_subln) * (1.0 - lam_init)
    return out.reshape(b, s, ATTN_WIDTH)


def _conformer_conv(glu_in, conv_w, conv_b, ln_g, ln_b, w_br):
    a, g = jnp.split(glu_in, 2, axis=-1)
    u = a * jax.nn.sigmoid(g)
    u = _causal_dwconv(u, conv_w, conv_b)
    u = _layernorm(u, ln_g, ln_b)
    u = jax.nn.silu(u)
    return u @ w_br


def setup_inputs(seed: int = 0) -> dict:
    key = jax.random.key(seed)
    ks = jax.random.split(key, 26)
    f32 = jnp.float32
    L, D = DEPTH, D_MODEL

    def nrm(k, shape, fan_in):
        return jax.random.normal(k, shape, f32) * (fan_in ** -0.5)

    def gain(k, shape):
        return 1.0 + 0.02 * jax.random.normal(k, shape, f32)

    return {
        "x": jax.random.normal(ks[0], (BATCH, SEQ, D), f32),
        "p": jax.random.normal(ks[1], (DEPTH, BATCH, SEQ, PLE_DIM), f32),
        "g_mix": gain(ks[2], (L, D)),
        "w_in": nrm(ks[3], (L, D, IN_COLS), D),
        "lam_q1": 0.1 * jax.random.normal(ks[4], (L, HEAD_DIM), f32),
        "lam_k1": 0.1 * jax.random.normal(ks[5], (L, HEAD_DIM), f32),
        "lam_q2": 0.1 * jax.random.normal(ks[6], (L, HEAD_DIM), f32),
        "lam_k2": 0.1 * jax.random.normal(ks[7], (L, HEAD_DIM), f32),
        "g_subln": gain(ks[8], (L, V_DIM)),
        "w_attn_br": nrm(ks[9], (L, ATTN_WIDTH, D), ATTN_WIDTH),
        "conv_w": nrm(ks[10], (L, CONV_K, CONV_CH), CONV_K),
        "conv_b": 0.02 * jax.random.normal(ks[11], (L, CONV_CH), f32),
        "ln_g": gain(ks[12], (L, CONV_CH)),
        "ln_b": 0.02 * jax.random.normal(ks[13], (L, CONV_CH), f32),
        "w_conv_br": nrm(ks[14], (L, CONV_CH, D), CONV_CH),
        "w_o": nrm(ks[15], (L, D, D), D),
        "g_ffn": gain(ks[16], (L, D)),
        "w_up": nrm(ks[17], (L, D, 2 * D_FF), D),
        "ffn_conv_w": nrm(ks[18], (L, FFN_CONV_K, 2 * D_FF), FFN_CONV_K),
        "ffn_conv_b": 0.02 * jax.random.normal(ks[19], (L, 2 * D_FF), f32),
        "w_down": nrm(ks[20], (L, D_FF, D), D_FF),
        "g_ple": gain(ks[21], (L, D)),
        "w_ple_gate": nrm(ks[22], (L, D, D), D),
        "w_ple_proj": nrm(ks[23], (L, PLE_DIM, D), PLE_DIM),
        "g_final": gain(ks[24], (D,)),
    }


def reference(x, p, g_mix, w_in, lam_q1, lam_k1, lam_q2, lam_k2, g_subln, w_attn_br,
              conv_w, conv_b, ln_g, ln_b, w_conv_br, w_o, g_ffn, w_up, ffn_conv_w,
              ffn_conv_b, w_down, g_ple, w_ple_gate, w_ple_proj, g_final):
    splits = np.cumsum([Q_COLS, K_COLS, V_COLS, GLU_COLS, D_MODEL]).tolist()
    for l in range(DEPTH):
        lam_init = 0.8 - 0.6 * math.exp(-0.3 * l)
        lam = (jnp.exp(jnp.sum(lam_q1[l].astype(jnp.float32) * lam_k1[l].astype(jnp.float32)))
               - jnp.exp(jnp.sum(lam_q2[l].astype(jnp.float32) * lam_k2[l].astype(jnp.float32)))
               + lam_init)
        h = _rmsnorm(x, g_mix[l])
        z = h @ w_in[l]
        hq, hk, hv, glu_in, gate_a, gate_c = jnp.split(z, splits, axis=-1)
        attn = _diff_attention(hq, hk, hv, lam, g_subln[l], lam_init) @ w_attn_br[l]
        conv = _conformer_conv(glu_in, conv_w[l], conv_b[l], ln_g[l], ln_b[l], w_conv_br[l])
        merged = jax.nn.sigmoid(gate_a) * attn + jax.nn.sigmoid(gate_c) * conv
        x = x + merged @ w_o[l]
        h = _rmsnorm(x, g_ffn[l])
        u = _causal_dwconv(h @ w_up[l], ffn_conv_w[l], ffn_conv_b[l])
        ug, uv = jnp.split(u, 2, axis=-1)
        x = x + (jax.nn.silu(ug) * uv) @ w_down[l]
        h = _rmsnorm(x, g_ple[l])
        x = x + jax.nn.sigmoid(h @ w_ple_gate[l]) * (p[l].astype(x.dtype) @ w_ple_proj[l])
    return _rmsnorm(x, g_final)
```

```python
import contextlib
import math
import numpy as np
import concourse.bass as bass
import concourse.mybir as mybir
from concourse.bass_utils import run_bass_kernel_spmd

F32 = mybir.dt.float32
BF16 = mybir.dt.bfloat16
AF = mybir.ActivationFunctionType
ALU = mybir.AluOpType

EPS = 1e-6
LN_EPS = 1e-5
CONV_K = 31
NEG_BIG = -1.0e6
SAME_ENGINE_SYNC = True
ALIBI_WIN = 64.0
PIPELINE_P2 = True


class Buf:
    __slots__ = ("name", "w", "r", "aliases", "sem_ld", "sem_st")

    def __init__(self, name):
        self.name = name
        self.w = None
        self.r = []
        self.aliases = []
        self.sem_ld = None
        self.sem_st = None


class Sched:
    ENGS = ["pe", "act", "dve", "pool", "sp"]

    def __init__(self):
        self.ops = {e: [] for e in self.ENGS}
        self.cnt = {e: 0 for e in self.ENGS}
        self.waited = {e: {} for e in self.ENGS}
        self.dma_sems = []
        self.dma_cnt = {}
        self.nbuf = 0

    def buf(self, name):
        self.nbuf += 1
        return Buf("%s_%d" % (name, self.nbuf))

    def _sem(self, name):
        self.dma_sems.append(name)
        self.dma_cnt[name] = 0
        return name

    def _wait(self, eng, deps):
        for d in deps:
            if d is None:
                continue
            key, val = d
            if key == eng and (eng in ("pe", "sp") or not SAME_ENGINE_SYNC):
                continue
            if self.waited[eng].get(key, 0) >= val:
                continue
            self.waited[eng][key] = val
            self.ops[eng].append(("wait", key, val))

    def _deps(self, reads, writes):
        deps = []
        for b in reads:
            deps.append(b.w)
            for a in b.aliases:
                deps.append(a.w)
        for b in writes:
            deps.append(b.w)
            deps.extend(b.r)
            for a in b.aliases:
                deps.append(a.w)
                deps.extend(a.r)
        return deps

    def _update(self, tok, reads, writes):
        for b in reads:
            b.r.append(tok)
            if len(b.r) > 24:
                last = {}
                for k, v in b.r:
                    if last.get(k, 0) < v:
                        last[k] = v
                b.r = list(last.items())
        for b in writes:
            b.w = tok
            b.r = []

    def op(self, eng, fn, reads=(), writes=(), extra=()):
        self._wait(eng, self._deps(reads, writes) + list(extra))
        self.cnt[eng] += 1
        self.ops[eng].append(("op", fn))
        tok = (eng, self.cnt[eng])
        self._update(tok, reads, writes)
        return tok

    def dma(self, eng, fn, reads=(), writes=(), extra=(), sem_buf=None, store=False):
        self._wait(eng, self._deps(reads, writes) + list(extra))
        b = sem_buf
        if store:
            if b.sem_st is None:
                b.sem_st = self._sem("st_" + b.name)
            sem = b.sem_st
        else:
            if b.sem_ld is None:
                b.sem_ld = self._sem("ld_" + b.name)
            sem = b.sem_ld
        self.dma_cnt[sem] += 16
        self.ops[eng].append(("dma", fn, sem))
        tok = (sem, self.dma_cnt[sem])
        self._update(tok, reads, writes)
        return tok

    def barrier(self):
        deps = [(e, self.cnt[e]) for e in self.ENGS if self.cnt[e] > 0]
        deps += [(sm, self.dma_cnt[sm]) for sm in self.dma_sems if self.dma_cnt[sm] > 0]
        for e in self.ENGS:
            self._wait(e, [d for d in deps if d[0] != e])

    def emit(self, nc):
        with contextlib.ExitStack() as st:
            sems = {}
            for e in self.ENGS:
                sems[e] = st.enter_context(nc.semaphore("c_" + e))
            for s in self.dma_sems:
                sems[s] = st.enter_context(nc.semaphore(s))
            block = st.enter_context(nc.Block())
            ops = self.ops

            def run(engobj, ename):
                for o in ops[ename]:
                    if o[0] == "wait":
                        engobj.wait_ge(sems[o[1]], o[2])
                    elif o[0] == "op":
                        o[1](engobj).then_inc(sems[ename], 1)
                    else:
                        o[1](engobj).then_inc(sems[o[2]], 16)

            @block.tensor
            def _(e):
                run(e, "pe")

            @block.scalar
            def _(e):
                run(e, "act")

            @block.vector
            def _(e):
                run(e, "dve")

            @block.gpsimd
            def _(e):
                run(e, "pool")

            @block.sync
            def _(e):
                run(e, "sp")


def make_cfg(D=2048, NH=8, CC=1024, DFF=5632, PLE=256, BLK=2048):
    c = dict(D=D, NH=NH, CC=CC, DFF=DFF, PLE=PLE, BLK=BLK)
    c["KD"] = D // 128
    c["KC"] = CC // 128
    c["KF"] = DFF // 128
    c["KP"] = PLE // 128
    c["TT"] = 512
    c["HAL"] = 32
    c["HQ"] = 32
    c["NTB"] = BLK // c["TT"]
    c["KTB"] = BLK // 128
    c["NSLOT"] = 8
    c["NKEY"] = 8 * BLK
    c["NKT"] = c["NKEY"] // 128
    c["G"] = 2 * (c["NTB"] + 1)
    c["NQ"] = 2 * (c["HQ"] + c["NTB"] * c["TT"])
    c["SEQ"] = 8 * BLK
    c["AW"] = NH * 128
    return c


SLOT_BLOCKS = {
    0: [0, 7, 1, 6, 2, 3, 4, 5],
    1: [1, 6, 0, 5, 2, 3, 4, -1],
    2: [2, 5, 1, 4, 0, 3, -1, -1],
    3: [3, 4, 2, 3, 0, 1, -1, -1],
}


def group_info(cfg):
    gs = []
    off = 0
    for side in (0, 1):
        gs.append(("h", side, 0, cfg["HQ"], off))
        off += cfg["HQ"]
        for i in range(cfg["NTB"]):
            gs.append(("o", side, i, cfg["TT"], off))
            off += cfg["TT"]
    return gs


def key_lists(cfg):
    KTB, NTB = cfg["KTB"], cfg["NTB"]
    out = []
    for (kind, side, i, ncols, off) in group_info(cfg):
        lst = []
        if kind == "o":
            own = side
            ctx = [2, 4, 5] if side == 0 else [0, 2, 3, 4, 5, 6, 7]
            for s in ctx:
                for t in range(KTB):
                    lst.append((s * KTB + t, "full", 0))
            for t in range(4 * (i + 1)):
                o = t - 4 * i
                if o < 0:
                    lst.append((own * KTB + t, "full", 0))
                else:
                    lst.append((own * KTB + t, "diag", o))
        else:
            dslot = 2 if side == 0 else 3
            ctx = [4, 5] if side == 0 else [0, 2, 4, 5, 6, 7]
            for s in ctx:
                for t in range(KTB):
                    lst.append((s * KTB + t, "full", 0))
            for t in range(KTB - 1):
                lst.append((dslot * KTB + t, "full", 0))
            lst.append((dslot * KTB + KTB - 1, "hdiag", 0))
        out.append(lst)
    return out


def cst_layout(cfg):
    KD, KC, KF, G, NKT = cfg["KD"], cfg["KC"], cfg["KF"], cfg["G"], cfg["NKT"]
    lay = {}
    off = 0
    for name, n in [("g_mix", KD), ("g_ffn", KD), ("g_ple", KD), ("g_final", KD),
                    ("conv_w", KC * CONV_K), ("conv_b", KC), ("ln_g", KC), ("ln_b", KC),
                    ("fcw", 2 * KF * 3), ("fcb", 2 * KF), ("gsub", 128), ("lamv", 256),
                    ("flag", 2), ("eps", 2), ("rp", G * NKT)]:
        lay[name] = (off, n)
        off += n
    lay["_n"] = off
    return lay


def I(method, *args, **kw):
    return lambda e: getattr(e, method)(*args, **kw)


def build_program(cfg, debug=False):
    D, NH, CC, DFF, PLE, BLK = cfg["D"], cfg["NH"], cfg["CC"], cfg["DFF"], cfg["PLE"], cfg["BLK"]
    KD, KC, KF, KP = cfg["KD"], cfg["KC"], cfg["KF"], cfg["KP"]
    TT, HAL, HQ, NTB, KTB = cfg["TT"], cfg["HAL"], cfg["HQ"], cfg["NTB"], cfg["KTB"]
    NKEY, NKT, G, NQ, AW = cfg["NKEY"], cfg["NKT"], cfg["G"], cfg["NQ"], cfg["AW"]
    NOWN = 2 * NTB * TT
    assert NH == KC
    slopes = [2.0 ** (-8.0 * (h + 1) / NH) for h in range(NH)]
    lam_init = 0.8 - 0.6 * math.exp(-0.3 * 0)
    lay = cst_layout(cfg)
    NCST = lay["_n"]
    groups = group_info(cfg)
    klists = key_lists(cfg)
    AX = mybir.AxisListType.X

    nc = bass.Bass("TRN2", target_bir_lowering=False)

    def din(name, shape, dt=F32):
        return nc.dram_tensor(name, list(shape), dt, kind="ExternalInput").ap()

    def dscr(name, shape, dt=BF16):
        kind = "ExternalOutput" if (debug and name.endswith("_s")) else "Internal"
        return nc.dram_tensor(name, list(shape), dt, kind=kind).ap()

    NCB = 128 + 128 + HQ + 128 + 128 + 4 * 512 + HQ
    xk = din("xk", [8 * NTB, 128, KD * TT])
    xo = din("xo", [2 * NTB, 128, KD * (HAL + TT)])
    xh = din("xh", [2, 128, KD * (HAL + HQ)])
    pt = din("pt", [2 * NTB, 128, KP * TT])
    cst_d = din("cst", [128, NCST])
    cstb_d = din("cstb", [128, NCB])
    dq_d = din("dq", [1, NQ])
    wq_d = din("wq", [128, NH * KD * 128])
    wk_d = din("wk", [128, NH * KD * 128])
    wv_d = din("wv", [128, KD * AW])
    wspec = [("glu", 2 * KC, KD), ("gate", 2 * KD, KD), ("abr", KD, NH), ("cbr", KD, KC), ("wo", KD, KD),
             ("up", 2 * KF, KD), ("down", KD, KF), ("pg", KD, KD), ("pp", KD, KP)]
    w32 = {}
    w16 = {}
    for nm, nch, kc in wspec:
        w32[nm] = din("w_" + nm, [nch * 128, kc * 128])
        w16[nm] = dscr("b_" + nm, [nch * 128, kc * 128])
    out_d = nc.dram_tensor("out", [KD * 128, NOWN], F32, kind="ExternalOutput").ap()
    kT_s = dscr("kT_s", [NH, 128, NKEY])
    v_s = dscr("v_s", [NKEY, AW])
    qT_s = dscr("qT_s", [NH, 128, NQ])
    aT_s = dscr("aT_s", [NH, 128, NQ])

    odbg_s = None
    if debug:
        odbg_s = nc.dram_tensor("odbg_s", [NH * G * 4 * 2 * 128, 129], F32, kind="ExternalOutput").ap()
    S = Sched()
    st = contextlib.ExitStack()
    with st:
        def sb(name, shape, dt):
            return st.enter_context(nc.sbuf_tensor("s_" + name, list(shape), dt))

        banks = [st.enter_context(nc.psum_tensor("bank%d" % i, [128, 512], F32)) for i in range(7)]
        bankT = st.enter_context(nc.psum_tensor("bankT", [128, 1024], BF16))
        bank_bufs = [S.buf("bank") for _ in range(7)]
        bankT_buf = S.buf("bankT")
        rot = {"i": 0}

        def next_bank():
            i = rot["i"] % 7
            rot["i"] += 1
            return banks[i], bank_bufs[i]

        cst = sb("cst", [128, NCST], F32)
        cstb = sb("cstb", [128, NCB], BF16)
        b_cst = S.buf("cst")
        b_cstb = S.buf("cstb")
        S.dma("sp", I("dma_start", out=cst[:], in_=cst_d), writes=[b_cst], sem_buf=b_cst)
        S.dma("pool", I("dma_start", out=cstb[:], in_=cstb_d), writes=[b_cstb], sem_buf=b_cstb)
        TRI = cstb[:, 0:128]
        IDENT = cstb[:, 128:256]
        MH = cstb[:, 256:256 + HQ]
        ONES_D = cstb[:, 256 + HQ:256 + HQ + 128]
        ONES_C = cstb[:, 256 + HQ + 128:256 + HQ + 256]
        MB0 = 256 + HQ + 256
        MASKD = [cstb[:, MB0 + o * 512:MB0 + (o + 1) * 512] for o in range(4)]
        MHN = cstb[:, MB0 + 2048:MB0 + 2048 + HQ]

        def cs(name, a=0, n=None):
            o, ln = lay[name]
            if n is None:
                n = ln - a
            return cst[:, o + a:o + a + n]

        EPSC = cs("eps", 0, 1)
        LNEPSC = cs("eps", 1, 1)
        b_wcast = S.buf("wcast")
        cast_jobs = []
        for nm, nch, kc in wspec:
            rows = nch * 128
            step = max(128, ((1 << 20) // (kc * 128)) // 128 * 128)
            r = 0
            while r < rows:
                r2 = min(rows, r + step)
                cast_jobs.append((nm, r, r2))
                r = r2
        store_toks = []
        gq = {}
        for gi, (kind, side, i, ncols, off) in enumerate(groups):
            gq[(kind, side, i)] = off

        p1 = contextlib.ExitStack()
        with p1:
            def sb1(name, shape, dt):
                return p1.enter_context(nc.sbuf_tensor("s_" + name, list(shape), dt))

            WK = sb1("WK", [128, NH * KD * 128], BF16)
            WV = sb1("WV", [128, KD * AW], BF16)
            b_WK, b_WV = S.buf("WK"), S.buf("WV")
            WQ, b_WQ = WK, b_WK
            S.dma("pool", I("dma_start", out=WK[:], in_=wk_d), writes=[b_WK], sem_buf=b_WK)
            S.dma("pool", I("dma_start", out=WV[:], in_=wv_d), writes=[b_WV], sem_buf=b_WV)
            for nm, r, r2 in cast_jobs:
                S.dma("pool", I("dma_start", out=w16[nm][r:r2, :], in_=w32[nm][r:r2, :]), sem_buf=b_wcast, store=True)
            tok_wcast = (b_wcast.sem_st, S.dma_cnt[b_wcast.sem_st])

            X1 = [sb1("X1_%d" % i, [128, KD * TT], F32) for i in range(2)]
            bX1 = [S.buf("X1") for _ in range(2)]
            H1 = [sb1("H1_%d" % i, [128, KD * TT], BF16) for i in range(2)]
            bH1 = [S.buf("H1") for _ in range(2)]
            SQ1 = [sb1("SQ1_%d" % i, [128, TT], BF16) for i in range(2)]
            bSQ1 = [S.buf("SQ1") for _ in range(2)]
            RS1 = [sb1("RS1_%d" % i, [128, TT], F32) for i in range(2)]
            bRS1 = [S.buf("RS1") for _ in range(2)]
            KS = [sb1("KS_%d" % i, [128, NH * TT], BF16) for i in range(2)]
            bKS = [S.buf("KS") for _ in range(2)]
            QS, bQS = KS, bKS
            VS = [sb1("VS_0", [128, (TT // 128) * AW], BF16)] * 2
            bVS = [S.buf("VS")] * 2
            c1 = {"sq": 0, "ev": 0, "rs": 0}

            def evac(out_ap, in_ap, rd, wr, scale=None):
                c1["ev"] += 1
                if scale is not None or c1["ev"] % 2 == 0:
                    sc = 1.0 if scale is None else scale
                    return S.op("act", I("activation", out=out_ap, in_=in_ap, func=AF.Copy, scale=sc), reads=rd, writes=wr)
                return S.op("dve", I("tensor_copy", out=out_ap, in_=in_ap), reads=rd, writes=wr)

            def rmsnorm1(X, bX, H, bH, W, gname):
                bk, bkb = next_bank()
                for kc in range(KD):
                    k = c1["sq"] % 2
                    c1["sq"] += 1
                    S.op("act", I("activation", out=SQ1[k][:, 0:W], in_=X[:, kc * W:(kc + 1) * W], func=AF.Square),
                         reads=[bX], writes=[bSQ1[k]])
                    S.op("pe", I("matmul", bk[:, 0:W], lhsT=ONES_D, rhs=SQ1[k][:, 0:W], start=(kc == 0), stop=(kc == KD - 1)),
                         reads=[bSQ1[k], b_cstb], writes=[bkb])
                r = c1["rs"] % 2
                c1["rs"] += 1
                S.op("act", I("activation", out=RS1[r][:, 0:W], in_=bk[:, 0:W], func=AF.Sqrt, bias=EPSC, scale=1.0),
                     reads=[bkb, b_cst], writes=[bRS1[r]])
                S.op("dve", I("reciprocal", out=RS1[r][:, 0:W], in_=RS1[r][:, 0:W]), writes=[bRS1[r]])
                for kc in range(KD):
                    S.op("dve", I("scalar_tensor_tensor", out=H[:, kc * W:(kc + 1) * W], in0=X[:, kc * W:(kc + 1) * W],
                                  scalar=cs(gname, kc, 1), in1=RS1[r][:, 0:W], op0=ALU.mult, op1=ALU.mult),
                         reads=[bX, bRS1[r], b_cst], writes=[bH])

            def p1_tile(idx, src_ap, W, do_kv, qcol, key0):
                k = idx % 2
                S.dma("sp", I("dma_start", out=X1[k][:, 0:KD * W].rearrange("p (c w) -> p c w", w=W), in_=src_ap),
                      writes=[bX1[k]], sem_buf=bX1[k])
                rmsnorm1(X1[k], bX1[k], H1[k], bH1[k], W, "g_mix")
                H = H1[k]
                if do_kv:
                    for hh in range(NH):
                        bk, bkb = next_bank()
                        for kc in range(KD):
                            S.op("pe", I("matmul", bk[:, 0:W], lhsT=WK[:, (hh * KD + kc) * 128:(hh * KD + kc + 1) * 128],
                                         rhs=H[:, kc * W:(kc + 1) * W], start=(kc == 0), stop=(kc == KD - 1)),
                                 reads=[bH1[k], b_WK], writes=[bkb])
                        evac(KS[k][:, hh * TT:hh * TT + W], bk[:, 0:W], [bkb], [bKS[k]])
                    t = S.dma("pool", I("dma_start", out=kT_s[:, :, key0:key0 + W].rearrange("h p w -> p h w"),
                                        in_=KS[k][:].rearrange("p (h w) -> p h w", w=TT)[:, :, 0:W]),
                              reads=[bKS[k]], sem_buf=bKS[k], store=True)
                    store_toks.append(t)
                    for stt in range(W // 128):
                        for c0 in range(0, AW, 512):
                            cw = min(512, AW - c0)
                            bk, bkb = next_bank()
                            for kc in range(KD):
                                S.op("pe", I("matmul", bk[:, 0:cw], lhsT=H[:, kc * W + stt * 128:kc * W + (stt + 1) * 128],
                                             rhs=WV[:, kc * AW + c0:kc * AW + c0 + cw], start=(kc == 0), stop=(kc == KD - 1)),
                                     reads=[bH1[k], b_WV], writes=[bkb])
                            evac(VS[k][:, stt * AW + c0:stt * AW + c0 + cw], bk[:, 0:cw], [bkb], [bVS[k]])
                    t = S.dma("pool", I("dma_start", out=v_s[key0:key0 + W, :].rearrange("(s p) c -> p s c", p=128),
                                        in_=VS[k][:].rearrange("p (s c) -> p s c", c=AW)[:, 0:W // 128, :]),
                              reads=[bVS[k]], sem_buf=bVS[k], store=True)
                    store_toks.append(t)
                if qcol is not None:
                    for hh in range(NH):
                        bk, bkb = next_bank()
                        for kc in range(KD):
                            S.op("pe", I("matmul", bk[:, 0:W], lhsT=WQ[:, (hh * KD + kc) * 128:(hh * KD + kc + 1) * 128],
                                         rhs=H[:, kc * W:(kc + 1) * W], start=(kc == 0), stop=(kc == KD - 1)),
                                 reads=[bH1[k], b_WQ], writes=[bkb])
                        evac(QS[k][:, hh * TT:hh * TT + W], bk[:, 0:W], [bkb], [bQS[k]], scale=0.125)
                    t = S.dma("pool", I("dma_start", out=qT_s[:, :, qcol:qcol + W].rearrange("h p w -> p h w"),
                                        in_=QS[k][:].rearrange("p (h w) -> p h w", w=TT)[:, :, 0:W]),
                              reads=[bQS[k]], sem_buf=bQS[k], store=True)
                    store_toks.append(t)

            idx = 0
            for s in range(8):
                for i in range(NTB):
                    src = xk[s * NTB + i].rearrange("p (c w) -> p c w", w=TT)
                    p1_tile(idx, src, TT, True, None, s * BLK + i * TT)
                    idx += 1
            S.dma("pool", I("dma_start", out=WQ[:], in_=wq_d), writes=[b_WQ], sem_buf=b_WQ)
            for s in range(2):
                for i in range(NTB):
                    src = xk[s * NTB + i].rearrange("p (c w) -> p c w", w=TT)
                    p1_tile(idx, src, TT, False, gq[("o", s, i)], 0)
                    idx += 1
            for side in (0, 1):
                src = xh[side].rearrange("p (c w) -> p c w", w=HAL + HQ)[:, :, HAL:HAL + HQ]
                p1_tile(idx, src, HQ, False, gq[("h", side, 0)], 0)
                idx += 1

        S.barrier()
        p2 = contextlib.ExitStack()
        with p2:
            def sb2(name, shape, dt):
                return p2.enter_context(nc.sbuf_tensor("s_" + name, list(shape), dt))

            KT = [sb2("KT%d" % n, [65, NKEY], BF16) for n in range(2)]
            bKT = [S.buf("KTs") for _ in range(8)]
            VT = sb2("VT", [128, NKT * 129], BF16)
            bVT = [S.buf("VTs") for _ in range(8)]
            QT = [sb2("QT%d" % n, [65, NQ], BF16) for n in range(2)]
            bQT = S.buf("QT")
            DQ = sb2("DQ", [65, NQ], F32)
            bDQ = S.buf("DQ")
            BIAS = [sb2("BIAS%d" % i, [128, G * NKT], F32) for i in range(2)]
            bBIAS = [S.buf("BIAS") for _ in range(2)]
            NPT = 3
            PT = [[sb2("PT%d_%d" % (n, i), [128, TT], BF16) for i in range(NPT)] for n in range(2)]
            bPT = [[S.buf("PT") for _ in range(NPT)] for n in range(2)]
            GS = sb2("GS", [128, 128], F32)
            bGS = S.buf("GS")
            LAMT = sb2("LAMT", [128, 8], F32)
            LAMJ = sb2("LAMJ", [128, 128], F32)
            bLAM = S.buf("LAM")
            EP = [sb2("EP%d" % i, [128, 8], F32) for i in range(2)]
            bEP = [S.buf("EP") for _ in range(2)]
            T0 = [sb2("T0_%d" % i, [128, 128], F32) for i in range(2)]
            bT0 = [S.buf("T0") for _ in range(2)]
            AA = [sb2("AA_%d" % i, [128, 128], F32) for i in range(2)]
            bAA = [S.buf("AA") for _ in range(2)]
            JK = sb2("JK", [128, 128], F32)
            bJK = S.buf("JK")
            AN = [sb2("AN_%d" % i, [128, 128], BF16) for i in range(2)]
            bAN = [S.buf("AN") for _ in range(2)]
            ATS = [sb2("ATS_%d" % i, [128, TT], BF16) for i in range(2)]
            bATS = [S.buf("ATS") for _ in range(2)]

            ZP = sb2("ZP", [128, 128], BF16)
            bZP = S.buf("ZP")
            S.op("dve", I("memset", ZP[:], 0.0), writes=[bZP])
            ODBG = sb2("ODBG", [128, 132], F32)
            bODBG = S.buf("ODBG")
            p1_done = list(store_toks)
            for n in range(2):
                S.op("dve", I("memset", KT[n][64:65, :], 1.0), writes=bKT)
            VT3 = VT[:].rearrange("p (t c) -> p t c", c=129)
            S.op("dve", I("memset", VT3[:, :, 128:129], 1.0), writes=bVT)
            S.dma("sp", I("dma_start", out=DQ[64:65, :], in_=dq_d), writes=[bDQ], sem_buf=bDQ)
            S.op("dve", I("tensor_scalar", out=GS[:], in0=cs("gsub"), scalar1=1.0 - lam_init, scalar2=None, op0=ALU.mult),
                 reads=[b_cst], writes=[bGS])
            lo = lay["lamv"][0]
            for t in range(2):
                S.op("dve", I("tensor_tensor", out=LAMJ[:, 0:64], in0=cst[:, lo + 128 * t:lo + 128 * t + 64],
                              in1=cst[:, lo + 128 * t + 64:lo + 128 * t + 128], op=ALU.mult), reads=[b_cst], writes=[bLAM])
                S.op("dve", I("reduce_sum", out=LAMT[:, t:t + 1], in_=LAMJ[:, 0:64], axis=AX), reads=[bLAM], writes=[bLAM])
            S.op("act", I("activation", out=LAMT[:, 2:4], in_=LAMT[:, 0:2], func=AF.Exp), reads=[bLAM], writes=[bLAM])
            S.op("dve", I("scalar_tensor_tensor", out=LAMT[:, 4:5], in0=LAMT[:, 3:4], scalar=-lam_init, in1=LAMT[:, 2:3],
                          op0=ALU.add, op1=ALU.subtract), reads=[bLAM], writes=[bLAM])
            NEG_LAM = LAMT[:, 4:5]

            Sb = [[banks[0], banks[1]], [banks[2], banks[3]]]
            bSb = [[bank_bufs[0], bank_bufs[1]], [bank_bufs[2], bank_bufs[3]]]

            def oreg(n, qt):
                r = n * 4 + qt
                return banks[4 + r // 3], (r % 3) * 132, bank_bufs[4 + r // 3]

            sctr = {"i": 0, "pt": 0, "ep": 0, "ats": 0}
            mind = min_dist_table(cfg)
            rp_o = lay["rp"][0]
            for h in range(NH):
                for s in range(8):
                    for n in range(2):
                        S.dma("sp", I("dma_start", out=KT[n][0:64, s * BLK:(s + 1) * BLK],
                                      in_=kT_s[h, n * 64:(n + 1) * 64, s * BLK:(s + 1) * BLK]),
                              writes=[bKT[s]], sem_buf=bKT[s], extra=p1_done)
                    S.dma("sp", I("dma_start", out=VT3[:, s * KTB:(s + 1) * KTB, 0:128],
                                  in_=v_s[s * BLK:(s + 1) * BLK, h * 128:(h + 1) * 128].rearrange("(t p) c -> p t c", p=128)),
                          writes=[bVT[s]], sem_buf=bVT[s], extra=p1_done)
                for n in range(2):
                    S.dma("sp", I("dma_start", out=QT[n][0:64, :], in_=qT_s[h, n * 64:(n + 1) * 64, :]),
                          writes=[bQT], sem_buf=bQT, extra=p1_done)
                p1_done = []
                for n in range(2):
                    S.op("dve", I("tensor_scalar", out=QT[n][64:65, :], in0=DQ[64:65, :], scalar1=slopes[h], scalar2=None,
                                  op0=ALU.mult), reads=[bDQ], writes=[bQT])
                bi = h % 2
                S.op("pool", I("tensor_scalar", out=BIAS[bi][:], in0=cst[:, rp_o:rp_o + G * NKT], scalar1=slopes[h],
                               scalar2=None, op0=ALU.mult), reads=[b_cst], writes=[bBIAS[bi]])

                for gi, (kind, side, i, ncols, qoff) in enumerate(groups):
                    lst = [e_ for e_ in klists[gi] if mind[gi][e_[0]] <= ALIBI_WIN / slopes[h]]
                    assert len(lst) > 0
                    nqt = max(1, ncols // 128)
                    Mq = min(128, ncols)
                    vis = []
                    for qt in range(nqt):
                        v = [ix for ix, (kt, kd, o) in enumerate(lst) if not (kd == "diag" and qt < o)]
                        vis.append((v[0], v[-1]))
                    pbs, sbis = [], []
                    for ix in range(len(lst)):
                        pbs.append(sctr["pt"] % NPT)
                        sctr["pt"] += 1
                        sbis.append(sctr["i"] % 2)
                        sctr["i"] += 1

                    def issue_S(ix):
                        kt, kd, o = lst[ix]
                        slot = kt // KTB
                        pb, sbi = pbs[ix], sbis[ix]
                        for n in range(2):
                            masked = kd in ("diag", "hdiag")
                            S.op("pe", I("matmul", Sb[n][sbi][:, 0:ncols], lhsT=KT[n][0:65, kt * 128:(kt + 1) * 128],
                                         rhs=QT[n][0:65, qoff:qoff + ncols], start=True, stop=not masked),
                                 reads=[bKT[slot], bQT], writes=[bSb[n][sbi]])
                            if masked:
                                mrhs = MASKD[o][:, 0:ncols] if kd == "diag" else MHN
                                S.op("pe", I("matmul", Sb[n][sbi][:, 0:ncols], lhsT=IDENT, rhs=mrhs, start=False, stop=True),
                                     reads=[b_cstb], writes=[bSb[n][sbi]])
                            S.op("act", I("activation", out=PT[n][pb][:, 0:ncols], in_=Sb[n][sbi][:, 0:ncols], func=AF.Exp,
                                          bias=BIAS[bi][:, gi * NKT + kt:gi * NKT + kt + 1], scale=1.0),
                                 reads=[bSb[n][sbi], bBIAS[bi]], writes=[bPT[n][pb]])

                    def issue_PV(ix):
                        kt, kd, o = lst[ix]
                        slot = kt // KTB
                        pb = pbs[ix]
                        for n in range(2):
                            for qt in range(nqt):
                                if kd == "diag" and qt < o:
                                    continue
                                ob, oc, obb = oreg(n, qt)
                                S.op("pe", I("matmul", ob[0:Mq, oc:oc + 129], lhsT=PT[n][pb][:, qt * 128:qt * 128 + Mq],
                                             rhs=VT3[:, kt, :], start=False, stop=(ix == vis[qt][1])),
                                     reads=[bPT[n][pb], bVT[slot]], writes=[obb])

                    kt0 = lst[0][0]
                    for n in range(2):
                        for qt in range(nqt):
                            ob, oc, obb = oreg(n, qt)
                            S.op("pe", I("matmul", ob[0:Mq, oc:oc + 129], lhsT=ZP[:, 0:Mq], rhs=VT3[:, kt0, :],
                                         start=True, stop=False), reads=[bZP, bVT[kt0 // KTB]], writes=[obb])
                    if PIPELINE_P2:
                        issue_S(0)
                        for ix in range(len(lst)):
                            if ix + 1 < len(lst):
                                issue_S(ix + 1)
                            issue_PV(ix)
                    else:
                        for ix in range(len(lst)):
                            issue_S(ix)
                            issue_PV(ix)
                    ai = sctr["ats"] % 2
                    sctr["ats"] += 1
                    M = Mq
                    for qt in range(nqt):
                        ob0, oc0, obb0 = oreg(0, qt)
                        ob1, oc1, obb1 = oreg(1, qt)
                        k = sctr["ep"] % 2
                        sctr["ep"] += 1
                        E = EP[k]
                        bE = bEP[k]
                        if debug:
                            for nn_, (obx, ocx, obbx) in enumerate(((ob0, oc0, obb0), (ob1, oc1, obb1))):
                                S.op("dve", I("tensor_copy", out=ODBG[0:M, 0:129], in_=obx[0:M, ocx:ocx + 129]),
                                     reads=[obbx], writes=[bODBG])
                                row0 = (((h * G + gi) * 4 + qt) * 2 + nn_) * 128
                                S.dma("pool", I("dma_start", out=odbg_s[row0:row0 + M, :], in_=ODBG[0:M, 0:129]),
                                      reads=[bODBG], sem_buf=bODBG, store=True)
                        S.op("dve", I("reciprocal", out=E[0:M, 0:1], in_=ob0[0:M, oc0 + 128:oc0 + 129]), reads=[obb0], writes=[bE])
                        S.op("dve", I("reciprocal", out=E[0:M, 1:2], in_=ob1[0:M, oc1 + 128:oc1 + 129]), reads=[obb1], writes=[bE])
                        S.op("dve", I("tensor_tensor", out=E[0:M, 2:3], in0=E[0:M, 1:2], in1=NEG_LAM[0:M, :], op=ALU.mult),
                             reads=[bLAM], writes=[bE])
                        S.op("dve", I("tensor_scalar", out=T0[k][0:M, :], in0=ob0[0:M, oc0:oc0 + 128], scalar1=E[0:M, 0:1],
                                      scalar2=None, op0=ALU.mult), reads=[obb0, bE], writes=[bT0[k]])
                        S.op("dve", I("scalar_tensor_tensor", out=AA[k][0:M, :], in0=ob1[0:M, oc1:oc1 + 128],
                                      scalar=E[0:M, 2:3], in1=T0[k][0:M, :], op0=ALU.mult, op1=ALU.add),
                             reads=[obb1, bE, bT0[k]], writes=[bAA[k]])
                        S.op("act", I("activation", out=JK[0:M, :], in_=AA[k][0:M, :], func=AF.Square, accum_out=E[0:M, 3:4]),
                             reads=[bAA[k]], writes=[bJK, bE])
                        S.op("act", I("activation", out=E[0:M, 4:5], in_=E[0:M, 3:4], func=AF.Sqrt, bias=EPSC[0:M, :],
                                      scale=1.0 / 128.0), reads=[bE, b_cst], writes=[bE])
                        S.op("dve", I("reciprocal", out=E[0:M, 5:6], in_=E[0:M, 4:5]), reads=[bE], writes=[bE])
                        S.op("dve", I("scalar_tensor_tensor", out=AN[k][0:M, :], in0=AA[k][0:M, :], scalar=E[0:M, 5:6],
                                      in1=GS[0:M, :], op0=ALU.mult, op1=ALU.mult), reads=[bAA[k], bE, bGS], writes=[bAN[k]])
                        S.op("pe", I("transpose", out=bankT[:, 0:M], in_=AN[k][0:M, :], identity=IDENT[0:M, 0:M]),
                             reads=[bAN[k], b_cstb], writes=[bankT_buf])
                        S.op("act", I("activation", out=ATS[ai][:, qt * 128:qt * 128 + M], in_=bankT[:, 0:M], func=AF.Copy),
                             reads=[bankT_buf], writes=[bATS[ai]])
                    t = S.dma("pool", I("dma_start", out=aT_s[h, :, qoff:qoff + ncols], in_=ATS[ai][:, 0:ncols]),
                              reads=[bATS[ai]], sem_buf=bATS[ai], store=True)
                    store_toks.append(t)
            p2_done = [(bATS[k_].sem_st, S.dma_cnt[bATS[k_].sem_st]) for k_ in range(2)]

        S.barrier()
        p3 = contextlib.ExitStack()
        with p3:
            def sb3(name, shape, dt):
                return p3.enter_context(nc.sbuf_tensor("s_" + name, list(shape), dt))

            W = HAL + TT
            XT = sb3("XT", [128, KD * W], F32)
            bXT = S.buf("XT")
            HT = sb3("HT", [128, KD * W], BF16)
            bHT = S.buf("HT")
            mix_elems = (3 * KC + KD) * TT + KC * W
            assert KF * TT <= mix_elems
            MIX = sb3("MIX", [128, mix_elems], BF16)
            AT = MIX[:, 0:KC * TT]
            CT = MIX[:, KC * TT:2 * KC * TT]
            CB = MIX[:, 2 * KC * TT:3 * KC * TT]
            MT = MIX[:, 3 * KC * TT:(3 * KC + KD) * TT]
            U = MIX[:, (3 * KC + KD) * TT:(3 * KC + KD) * TT + KC * W]
            ACTF = MIX[:, 0:KF * TT]
            bAT, bCT, bCB, bMT, bU, bACTF = S.buf("AT"), S.buf("CT"), S.buf("CB"), S.buf("MT"), S.buf("U"), S.buf("ACTF")
            for b in (bAT, bCT, bCB, bMT, bU):
                b.aliases.append(bACTF)
                bACTF.aliases.append(b)
            PTb = sb3("PTb", [128, KP * TT], BF16)
            bPTb = S.buf("PTb")
            NWS = 4
            KMAX = max(KD, KF)
            WS = [sb3("WS%d" % i, [128, KMAX * 128], BF16) for i in range(NWS)]
            bWS = [S.buf("WS") for _ in range(NWS)]
            ACC = [sb3("ACC%d" % i, [128, TT], F32) for i in range(2)]
            bACC = [S.buf("ACC") for _ in range(2)]
            SQ3 = [sb3("SQ3_%d" % i, [128, TT], BF16) for i in range(2)]
            bSQ3 = [S.buf("SQ3") for _ in range(2)]
            RS3 = sb3("RS3", [128, TT], F32)
            bRS3 = S.buf("RS3")
            LNM = sb3("LNM", [128, TT], F32)
            LNR = sb3("LNR", [128, TT], F32)
            LNT = sb3("LNT", [128, TT], F32)
            bLNM, bLNR, bLNT = S.buf("LNM"), S.buf("LNR"), S.buf("LNT")
            SG = [sb3("SG%d" % i, [128, TT], F32) for i in range(4)]
            bSG = [S.buf("SG") for _ in range(4)]
            TMP = [sb3("TMP%d" % i, [128, TT], F32) for i in range(2)]
            bTMP = [S.buf("TMP") for _ in range(2)]
            UX = [sb3("UX%d" % i, [128, TT + 2], F32) for i in range(4)]
            bUX = [S.buf("UX") for _ in range(4)]
            CARRY = sb3("CARRY", [128, 2 * KF * 2], F32)
            bCARRY = S.buf("CARRY")
            OUTR = [sb3("OUTR%d" % i, [128, TT], F32) for i in range(2)]
            bOUTR = [S.buf("OUTR") for _ in range(2)]
            ctr = {"sq": 0, "sg": 0, "tmp": 0, "ux": 0, "acc": 0, "out": 0}

            def tile_wseq(is_halo):
                seq = []
                for c in range(2 * KC):
                    seq.append(("glu", c, KD))
                for oc in range(KD):
                    seq.append(("gate", oc, KD))
                    seq.append(("gate", KD + oc, KD))
                    seq.append(("abr", oc, NH))
                    seq.append(("cbr", oc, KC))
                for oc in range(KD):
                    seq.append(("wo", oc, KD))
                for f in range(KF):
                    seq.append(("up", f, KD))
                    seq.append(("up", KF + f, KD))
                if not is_halo:
                    for oc in range(KD):
                        seq.append(("down", oc, KF))
                    for oc in range(KD):
                        seq.append(("pg", oc, KD))
                        seq.append(("pp", oc, KP))
                return seq

            tiles = []
            for side in (0, 1):
                tiles.append(("h", side, 0))
                for i in range(NTB):
                    tiles.append(("o", side, i))
            wuses = []
            for (kind, side, i) in tiles:
                wuses.extend(tile_wseq(kind == "h"))
            wstate = {"loaded": 0, "use": 0}

            def wload_upto(u):
                while wstate["loaded"] <= u and wstate["loaded"] < len(wuses):
                    v = wstate["loaded"]
                    nm, ch, kcn = wuses[v]
                    sl = v % NWS
                    S.dma("sp", I("dma_start", out=WS[sl][:, 0:kcn * 128], in_=w16[nm][ch * 128:(ch + 1) * 128, :]),
                          writes=[bWS[sl]], sem_buf=bWS[sl], extra=[tok_wcast])
                    wstate["loaded"] += 1

            def wnext(nm, ch):
                u = wstate["use"]
                assert wuses[u][0] == nm and wuses[u][1] == ch, (wuses[u], nm, ch)
                wload_upto(u + NWS - 1)
                wstate["use"] += 1
                sl = u % NWS
                return WS[sl], bWS[sl]

            def mm_group(bk, bkb, n1, wt, wb, kcn, rhs_fn, rhs_bufs):
                for kc in range(kcn):
                    S.op("pe", I("matmul", bk[:, 0:n1], lhsT=wt[:, kc * 128:(kc + 1) * 128], rhs=rhs_fn(kc),
                                 start=(kc == 0), stop=(kc == kcn - 1)), reads=[wb] + rhs_bufs, writes=[bkb])

            def rmsnorm3(c0, c1e):
                Wn = c1e - c0
                bk, bkb = next_bank()
                for kc in range(KD):
                    k = ctr["sq"] % 2
                    ctr["sq"] += 1
                    S.op("act", I("activation", out=SQ3[k][:, 0:Wn], in_=XT[:, kc * W + c0:kc * W + c1e], func=AF.Square),
                         reads=[bXT], writes=[bSQ3[k]])
                    S.op("pe", I("matmul", bk[:, 0:Wn], lhsT=ONES_D, rhs=SQ3[k][:, 0:Wn], start=(kc == 0), stop=(kc == KD - 1)),
                         reads=[bSQ3[k], b_cstb], writes=[bkb])
                S.op("act", I("activation", out=RS3[:, 0:Wn], in_=bk[:, 0:Wn], func=AF.Sqrt, bias=EPSC, scale=1.0),
                     reads=[bkb, b_cst], writes=[bRS3])
                S.op("dve", I("reciprocal", out=RS3[:, 0:Wn], in_=RS3[:, 0:Wn]), writes=[bRS3])

            def apply_norm(c0, c1e, gname):
                Wn = c1e - c0
                for kc in range(KD):
                    S.op("dve", I("scalar_tensor_tensor", out=HT[:, kc * W + c0:kc * W + c1e],
                                  in0=XT[:, kc * W + c0:kc * W + c1e], scalar=cs(gname, kc, 1), in1=RS3[:, 0:Wn],
                                  op0=ALU.mult, op1=ALU.mult), reads=[bXT, bRS3, b_cst], writes=[bHT])

            XT3 = XT[:].rearrange("p (c w) -> p c w", w=W)
            AT3 = AT.rearrange("p (c w) -> p c w", w=TT)
            PT3 = PTb[:].rearrange("p (c w) -> p c w", w=TT)
            cwo = lay["conv_w"][0]
            fwo = lay["fcw"][0]
            fl = lay["flag"][0]
            own_idx = 0
            for ti, (kind, side, i) in enumerate(tiles):
                is_halo = kind == "h"
                T = HQ if is_halo else TT
                Wt = HAL + T
                if is_halo:
                    src = xh[side].rearrange("p (c w) -> p c w", w=Wt)
                    qoff = gq[("h", side, 0)]
                else:
                    src = xo[side * NTB + i].rearrange("p (c w) -> p c w", w=Wt)
                    qoff = gq[("o", side, i)]
                S.dma("pool", I("dma_start", out=XT3[:, :, 0:Wt], in_=src), writes=[bXT], sem_buf=bXT)
                S.dma("pool", I("dma_start", out=AT3[:, :, 0:T], in_=aT_s[:, :, qoff:qoff + T].rearrange("h p w -> p h w")),
                      writes=[bAT], sem_buf=bAT, extra=p2_done)
                p2_done = []
                if not is_halo:
                    S.dma("pool", I("dma_start", out=PT3, in_=pt[side * NTB + i].rearrange("p (c w) -> p c w", w=TT)),
                          writes=[bPTb], sem_buf=bPTb)
                for (c0, c1e) in ((0, HAL), (HAL, Wt)):
                    rmsnorm3(c0, c1e)
                    apply_norm(c0, c1e, "g_mix")

                def ht_main(kc):
                    return HT[:, kc * W + HAL:kc * W + HAL + T]

                def ht_halo(kc):
                    return HT[:, kc * W:kc * W + HAL]

                for c in range(KC):
                    wt, wb = wnext("glu", c)
                    bk, bkb = next_bank()
                    mm_group(bk, bkb, T, wt, wb, KD, ht_main, [bHT])
                    bk2, bkb2 = next_bank()
                    mm_group(bk2, bkb2, HAL, wt, wb, KD, ht_halo, [bHT])
                    S.op("act", I("activation", out=U[:, c * W + HAL:c * W + HAL + T], in_=bk[:, 0:T], func=AF.Copy),
                         reads=[bkb], writes=[bU])
                    S.op("act", I("activation", out=U[:, c * W:c * W + HAL], in_=bk2[:, 0:HAL], func=AF.Copy),
                         reads=[bkb2], writes=[bU])
                for c in range(KC):
                    wt, wb = wnext("glu", KC + c)
                    bk, bkb = next_bank()
                    mm_group(bk, bkb, T, wt, wb, KD, ht_main, [bHT])
                    bk2, bkb2 = next_bank()
                    mm_group(bk2, bkb2, HAL, wt, wb, KD, ht_halo, [bHT])
                    k = ctr["sg"] % 4
                    ctr["sg"] += 1
                    S.op("act", I("activation", out=SG[k][:, 0:T], in_=bk[:, 0:T], func=AF.Sigmoid), reads=[bkb], writes=[bSG[k]])
                    S.op("dve", I("tensor_tensor", out=U[:, c * W + HAL:c * W + HAL + T], in0=U[:, c * W + HAL:c * W + HAL + T],
                                  in1=SG[k][:, 0:T], op=ALU.mult), reads=[bSG[k]], writes=[bU])
                    k2 = ctr["sg"] % 4
                    ctr["sg"] += 1
                    S.op("act", I("activation", out=SG[k2][:, 0:HAL], in_=bk2[:, 0:HAL], func=AF.Sigmoid),
                         reads=[bkb2], writes=[bSG[k2]])
                    S.op("dve", I("tensor_tensor", out=U[:, c * W:c * W + HAL], in0=U[:, c * W:c * W + HAL],
                                  in1=SG[k2][:, 0:HAL], op=ALU.mult), reads=[bSG[k2]], writes=[bU])
                bkm, bkmb = next_bank()
                bkq, bkqb = next_bank()
                for c in range(KC):
                    a = ctr["acc"] % 2
                    ctr["acc"] += 1
                    b0 = c * W + HAL - (CONV_K - 1)
                    S.op("dve", I("tensor_scalar", out=ACC[a][:, 0:T], in0=U[:, b0:b0 + T],
                                  scalar1=cst[:, cwo + c * CONV_K:cwo + c * CONV_K + 1], scalar2=cs("conv_b", c, 1),
                                  op0=ALU.mult, op1=ALU.add), reads=[bU, b_cst], writes=[bACC[a]])
                    for kk in range(1, CONV_K):
                        S.op("dve", I("scalar_tensor_tensor", out=ACC[a][:, 0:T], in0=U[:, b0 + kk:b0 + kk + T],
                                      scalar=cst[:, cwo + c * CONV_K + kk:cwo + c * CONV_K + kk + 1], in1=ACC[a][:, 0:T],
                                      op0=ALU.mult, op1=ALU.add), reads=[bU], writes=[bACC[a]])
                    S.op("act", I("activation", out=CB[:, c * TT:c * TT + T], in_=ACC[a][:, 0:T], func=AF.Copy),
                         reads=[bACC[a]], writes=[bCB])
                    q = ctr["sq"] % 2
                    ctr["sq"] += 1
                    S.op("act", I("activation", out=SQ3[q][:, 0:T], in_=ACC[a][:, 0:T], func=AF.Square),
                         reads=[bACC[a]], writes=[bSQ3[q]])
                    S.op("pe", I("matmul", bkm[:, 0:T], lhsT=ONES_C, rhs=CB[:, c * TT:c * TT + T], start=(c == 0), stop=(c == KC - 1)),
                         reads=[bCB, b_cstb], writes=[bkmb])
                    S.op("pe", I("matmul", bkq[:, 0:T], lhsT=ONES_C, rhs=SQ3[q][:, 0:T], start=(c == 0), stop=(c == KC - 1)),
                         reads=[bSQ3[q], b_cstb], writes=[bkqb])
                S.op("act", I("activation", out=LNM[:, 0:T], in_=bkm[:, 0:T], func=AF.Copy), reads=[bkmb], writes=[bLNM])
                S.op("dve", I("tensor_tensor", out=LNT[:, 0:T], in0=LNM[:, 0:T], in1=LNM[:, 0:T], op=ALU.mult),
                     reads=[bLNM], writes=[bLNT])
                S.op("dve", I("tensor_tensor", out=LNT[:, 0:T], in0=bkq[:, 0:T], in1=LNT[:, 0:T], op=ALU.subtract),
                     reads=[bkqb], writes=[bLNT])
                S.op("act", I("activation", out=LNR[:, 0:T], in_=LNT[:, 0:T], func=AF.Sqrt, bias=LNEPSC, scale=1.0),
                     reads=[bLNT, b_cst], writes=[bLNR])
                S.op("dve", I("reciprocal", out=LNR[:, 0:T], in_=LNR[:, 0:T]), writes=[bLNR])
                for c in range(KC):
                    k = ctr["tmp"] % 2
                    ctr["tmp"] += 1
                    S.op("dve", I("tensor_tensor", out=TMP[k][:, 0:T], in0=CB[:, c * TT:c * TT + T], in1=LNM[:, 0:T],
                                  op=ALU.subtract), reads=[bCB, bLNM], writes=[bTMP[k]])
                    S.op("dve", I("tensor_tensor", out=TMP[k][:, 0:T], in0=TMP[k][:, 0:T], in1=LNR[:, 0:T], op=ALU.mult),
                         reads=[bLNR], writes=[bTMP[k]])
                    S.op("act", I("activation", out=CT[:, c * TT:c * TT + T], in_=TMP[k][:, 0:T], func=AF.Silu,
                                  bias=cs("ln_b", c, 1), scale=cs("ln_g", c, 1)), reads=[bTMP[k], b_cst], writes=[bCT])
                for oc in range(KD):
                    wt, wb = wnext("gate", oc)
                    bga, bgab = next_bank()
                    mm_group(bga, bgab, T, wt, wb, KD, ht_main, [bHT])
                    wt, wb = wnext("gate", KD + oc)
                    bgc, bgcb = next_bank()
                    mm_group(bgc, bgcb, T, wt, wb, KD, ht_main, [bHT])
                    wt, wb = wnext("abr", oc)
                    bat, batb = next_bank()
                    mm_group(bat, batb, T, wt, wb, NH, lambda kc: AT[:, kc * TT:kc * TT + T], [bAT])
                    wt, wb = wnext("cbr", oc)
                    bcv, bcvb = next_bank()
                    mm_group(bcv, bcvb, T, wt, wb, KC, lambda kc: CT[:, kc * TT:kc * TT + T], [bCT])
                    ka = ctr["sg"] % 4
                    kc_ = (ctr["sg"] + 1) % 4
                    ctr["sg"] += 2
                    S.op("act", I("activation", out=SG[ka][:, 0:T], in_=bga[:, 0:T], func=AF.Sigmoid), reads=[bgab], writes=[bSG[ka]])
                    S.op("act", I("activation", out=SG[kc_][:, 0:T], in_=bgc[:, 0:T], func=AF.Sigmoid), reads=[bgcb], writes=[bSG[kc_]])
                    S.op("dve", I("tensor_tensor", out=SG[ka][:, 0:T], in0=bat[:, 0:T], in1=SG[ka][:, 0:T], op=ALU.mult),
                         reads=[batb], writes=[bSG[ka]])
                    S.op("dve", I("tensor_tensor", out=SG[kc_][:, 0:T], in0=bcv[:, 0:T], in1=SG[kc_][:, 0:T], op=ALU.mult),
                         reads=[bcvb], writes=[bSG[kc_]])
                    S.op("dve", I("tensor_tensor", out=MT[:, oc * TT:oc * TT + T], in0=SG[ka][:, 0:T], in1=SG[kc_][:, 0:T],
                                  op=ALU.add), reads=[bSG[ka], bSG[kc_]], writes=[bMT])
                for oc in range(KD):
                    wt, wb = wnext("wo", oc)
                    bk, bkb = next_bank()
                    mm_group(bk, bkb, T, wt, wb, KD, lambda kc: MT[:, kc * TT:kc * TT + T], [bMT])
                    S.op("dve", I("tensor_tensor", out=XT[:, oc * W + HAL:oc * W + HAL + T],
                                  in0=XT[:, oc * W + HAL:oc * W + HAL + T], in1=bk[:, 0:T], op=ALU.add),
                         reads=[bkb], writes=[bXT])
                rmsnorm3(HAL, Wt)
                apply_norm(HAL, Wt, "g_ffn")

                def ffn_half(f, bk, bkb):
                    u = ctr["ux"] % 4
                    ctr["ux"] += 1
                    S.op("act", I("activation", out=UX[u][:, 2:2 + T], in_=bk[:, 0:T], func=AF.Copy), reads=[bkb], writes=[bUX[u]])
                    S.op("pool", I("tensor_copy", out=UX[u][:, 0:2], in_=CARRY[:, 2 * f:2 * f + 2]),
                         reads=[bCARRY], writes=[bUX[u]])
                    S.op("pool", I("tensor_copy", out=CARRY[:, 2 * f:2 * f + 2], in_=UX[u][:, T:T + 2]),
                         reads=[bUX[u]], writes=[bCARRY])
                    a = ctr["tmp"] % 2
                    ctr["tmp"] += 1
                    S.op("dve", I("tensor_scalar", out=TMP[a][:, 0:T], in0=UX[u][:, 0:T],
                                  scalar1=cst[:, fwo + 3 * f:fwo + 3 * f + 1], scalar2=cs("fcb", f, 1),
                                  op0=ALU.mult, op1=ALU.add), reads=[bUX[u], b_cst], writes=[bTMP[a]])
                    for kk in (1, 2):
                        S.op("dve", I("scalar_tensor_tensor", out=TMP[a][:, 0:T], in0=UX[u][:, kk:kk + T],
                                      scalar=cst[:, fwo + 3 * f + kk:fwo + 3 * f + kk + 1], in1=TMP[a][:, 0:T],
                                      op0=ALU.mult, op1=ALU.add), reads=[bUX[u]], writes=[bTMP[a]])
                    return a

                for f in range(KF):
                    wt, wb = wnext("up", f)
                    bg, bgb = next_bank()
                    mm_group(bg, bgb, T, wt, wb, KD, ht_main, [bHT])
                    wt, wb = wnext("up", KF + f)
                    bv, bvb = next_bank()
                    mm_group(bv, bvb, T, wt, wb, KD, ht_main, [bHT])
                    if is_halo:
                        for ff, bk, bkb in ((f, bg, bgb), (KF + f, bv, bvb)):
                            S.op("dve", I("tensor_scalar", out=CARRY[:, 2 * ff:2 * ff + 2], in0=bk[:, T - 2:T],
                                          scalar1=cst[:, fl + side:fl + side + 1], scalar2=None, op0=ALU.mult),
                                 reads=[bkb, b_cst], writes=[bCARRY])
                        continue
                    ag = ffn_half(f, bg, bgb)
                    k = ctr["sg"] % 4
                    ctr["sg"] += 1
                    S.op("act", I("activation", out=SG[k][:, 0:T], in_=TMP[ag][:, 0:T], func=AF.Silu),
                         reads=[bTMP[ag]], writes=[bSG[k]])
                    av = ffn_half(KF + f, bv, bvb)
                    S.op("dve", I("tensor_tensor", out=ACTF[:, f * TT:f * TT + T], in0=SG[k][:, 0:T], in1=TMP[av][:, 0:T],
                                  op=ALU.mult), reads=[bSG[k], bTMP[av]], writes=[bACTF])
                if is_halo:
                    continue
                for oc in range(KD):
                    wt, wb = wnext("down", oc)
                    bk, bkb = next_bank()
                    mm_group(bk, bkb, T, wt, wb, KF, lambda kc: ACTF[:, kc * TT:kc * TT + T], [bACTF])
                    S.op("dve", I("tensor_tensor", out=XT[:, oc * W + HAL:oc * W + HAL + T],
                                  in0=XT[:, oc * W + HAL:oc * W + HAL + T], in1=bk[:, 0:T], op=ALU.add),
                         reads=[bkb], writes=[bXT])
                rmsnorm3(HAL, Wt)
                apply_norm(HAL, Wt, "g_ple")
                for oc in range(KD):
                    wt, wb = wnext("pg", oc)
                    bg, bgb = next_bank()
                    mm_group(bg, bgb, T, wt, wb, KD, ht_main, [bHT])
                    wt, wb = wnext("pp", oc)
                    bp, bpb = next_bank()
                    mm_group(bp, bpb, T, wt, wb, KP, lambda kc: PTb[:, kc * TT:kc * TT + T], [bPTb])
                    k = ctr["sg"] % 4
                    ctr["sg"] += 1
                    S.op("act", I("activation", out=SG[k][:, 0:T], in_=bg[:, 0:T], func=AF.Sigmoid), reads=[bgb], writes=[bSG[k]])
                    S.op("dve", I("tensor_tensor", out=SG[k][:, 0:T], in0=bp[:, 0:T], in1=SG[k][:, 0:T], op=ALU.mult),
                         reads=[bpb], writes=[bSG[k]])
                    S.op("dve", I("tensor_tensor", out=XT[:, oc * W + HAL:oc * W + HAL + T],
                                  in0=XT[:, oc * W + HAL:oc * W + HAL + T], in1=SG[k][:, 0:T], op=ALU.add),
                         reads=[bSG[k]], writes=[bXT])
                rmsnorm3(HAL, Wt)
                col0 = own_idx * TT
                own_idx += 1
                for oc in range(KD):
                    k = ctr["out"] % 2
                    ctr["out"] += 1
                    S.op("dve", I("scalar_tensor_tensor", out=OUTR[k][:, 0:T], in0=XT[:, oc * W + HAL:oc * W + HAL + T],
                                  scalar=cs("g_final", oc, 1), in1=RS3[:, 0:T], op0=ALU.mult, op1=ALU.mult),
                         reads=[bXT, bRS3, b_cst], writes=[bOUTR[k]])
                    S.dma("pool", I("dma_start", out=out_d[oc * 128:(oc + 1) * 128, col0:col0 + TT], in_=OUTR[k][:, 0:T]),
                          reads=[bOUTR[k]], sem_buf=bOUTR[k], store=True)
            final_deps = [(bOUTR[k_].sem_st, S.dma_cnt[bOUTR[k_].sem_st]) for k_ in range(2)]
            S.op("pool", I("nop"), extra=final_deps)
        S.emit(nc)
    return nc


def _fm_tile(xrows, KD):
    W = xrows.shape[0]
    t = xrows.T.reshape(KD, 128, W).transpose(1, 0, 2)
    return np.ascontiguousarray(t).reshape(128, KD * W)


def _lhsT_chunks(Wm):
    K, N = Wm.shape
    t = Wm.reshape(K // 128, 128, N // 128, 128).transpose(2, 1, 0, 3)
    return np.ascontiguousarray(t).reshape(N // 128 * 128, K)


def _col(v, n):
    return np.ascontiguousarray(v.reshape(n, 128).T)


def rp_table(cfg, j):
    BLK, TT, HQ, NTB, KTB, NKT, G = cfg["BLK"], cfg["TT"], cfg["HQ"], cfg["NTB"], cfg["KTB"], cfg["NKT"], cfg["G"]
    sblk = SLOT_BLOCKS[j]
    A, B = j, 7 - j
    rp = np.full((128, G, NKT), NEG_BIG, np.float32)
    pp = np.arange(128)
    for gi, (kind, side, i, ncols, off) in enumerate(group_info(cfg)):
        blk = A if side == 0 else B
        if kind == "o":
            qblk = blk
            qref = blk * BLK + i * TT + TT - 1
            own_slot = side
        else:
            qblk = blk - 1
            qref = blk * BLK - 1
            own_slot = 2 if side == 0 else 3
        for s in range(8):
            sb_ = sblk[s]
            if kind == "h" and qblk < 0:
                if s == own_slot:
                    for t in range(KTB):
                        rp[:, gi, s * KTB + t] = (t * 128 + pp) - (KTB * 128 - 1)
                continue
            if sb_ < 0:
                continue
            if s == own_slot:
                vis = True
            else:
                vis = sb_ < qblk
                if side == 1 and s == 0 and sblk[0] == sblk[3]:
                    vis = False
            if not vis:
                continue
            for t in range(KTB):
                rp[:, gi, s * KTB + t] = (sb_ * BLK + t * 128 + pp) - qref
    return rp


def min_dist_table(cfg):
    G, NKT = cfg["G"], cfg["NKT"]
    md = np.full((G, NKT), np.inf)
    gi_info = group_info(cfg)
    for j in range(4):
        rp = rp_table(cfg, j)
        for gi, (kind, side, i, ncols, off) in enumerate(gi_info):
            r = rp[127, gi, :].astype(np.float64)
            vis = r > NEG_BIG / 2
            dist = -r - (ncols - 1)
            md[gi, vis] = np.minimum(md[gi, vis], dist[vis])
    return md


def prep_core_inputs(cfg, inputs, shared, c):
    D, NH, BLK = cfg["D"], cfg["NH"], cfg["BLK"]
    KD, KC, KF, KP = cfg["KD"], cfg["KC"], cfg["KF"], cfg["KP"]
    TT, HAL, HQ, NTB, KTB, NKT, G = cfg["TT"], cfg["HAL"], cfg["HQ"], cfg["NTB"], cfg["KTB"], cfg["NKT"], cfg["G"]
    b, j = c // 4, c % 4
    x = inputs["x"][b]
    p = inputs["p"][0, b]
    sblk = SLOT_BLOCKS[j]
    A, B = j, 7 - j
    m = {}
    xk = np.zeros((8 * NTB, 128, KD * TT), np.float32)
    for s in range(8):
        if sblk[s] < 0:
            continue
        for i in range(NTB):
            t0 = sblk[s] * BLK + i * TT
            xk[s * NTB + i] = _fm_tile(x[t0:t0 + TT], KD)
    m["xk"] = xk

    def rows(t0, t1):
        out = np.zeros((t1 - t0, D), np.float32)
        a0 = max(t0, 0)
        if t1 > a0:
            out[a0 - t0:] = x[a0:t1]
        return out

    xo = np.zeros((2 * NTB, 128, KD * (HAL + TT)), np.float32)
    xh = np.zeros((2, 128, KD * (HAL + HQ)), np.float32)
    ptl = np.zeros((2 * NTB, 128, KP * TT), np.float32)
    for side, blk in ((0, A), (1, B)):
        xh[side] = _fm_tile(rows(blk * BLK - HQ - HAL, blk * BLK), KD)
        for i in range(NTB):
            t0 = blk * BLK + i * TT
            xo[side * NTB + i] = _fm_tile(rows(t0 - HAL, t0 + TT), KD)
            ptl[side * NTB + i] = _fm_tile(p[t0:t0 + TT], KP)
    m["xo"], m["xh"], m["pt"] = xo, xh, ptl

    lay = cst_layout(cfg)
    cst = shared["cst_base"].copy()
    rp = rp_table(cfg, j)
    o, n = lay["rp"]
    cst[:, o:o + n] = rp.reshape(128, G * NKT)
    o, n = lay["flag"]
    cst[:, o] = 1.0 if A > 0 else 0.0
    cst[:, o + 1] = 1.0
    m["cst"] = cst
    for k in ("cstb", "dq", "wq", "wk", "wv"):
        m[k] = shared[k]
    for k in shared["w3"]:
        m["w_" + k] = shared["w3"][k]
    return m


def prep_shared(cfg, inputs):
    D, NH, CC, DFF, PLE = cfg["D"], cfg["NH"], cfg["CC"], cfg["DFF"], cfg["PLE"]
    KD, KC, KF, KP = cfg["KD"], cfg["KC"], cfg["KF"], cfg["KP"]
    TT, HQ, NTB, NQ, AW = cfg["TT"], cfg["HQ"], cfg["NTB"], cfg["NQ"], cfg["AW"]
    lay = cst_layout(cfg)
    cst = np.zeros((128, lay["_n"]), np.float32)

    def put(name, arr):
        o, n = lay[name]
        cst[:, o:o + n] = arr.reshape(128, n)

    put("g_mix", _col(inputs["g_mix"][0], KD))
    put("g_ffn", _col(inputs["g_ffn"][0], KD))
    put("g_ple", _col(inputs["g_ple"][0], KD))
    put("g_final", _col(inputs["g_final"], KD))
    cw = inputs["conv_w"][0]
    put("conv_w", np.ascontiguousarray(cw.T.reshape(KC, 128, CONV_K).transpose(1, 0, 2)))
    put("conv_b", _col(inputs["conv_b"][0], KC))
    put("ln_g", _col(inputs["ln_g"][0], KC))
    put("ln_b", _col(inputs["ln_b"][0], KC))
    fw = inputs["ffn_conv_w"][0]
    put("fcw", np.ascontiguousarray(fw.T.reshape(2 * KF, 128, 3).transpose(1, 0, 2)))
    put("fcb", _col(inputs["ffn_conv_b"][0], 2 * KF))
    put("gsub", np.broadcast_to(inputs["g_subln"][0][None, :], (128, 128)))
    lamv = np.concatenate([inputs["lam_q1"][0], inputs["lam_k1"][0], inputs["lam_q2"][0], inputs["lam_k2"][0]])
    put("lamv", np.broadcast_to(lamv[None, :], (128, 256)))
    o_, n_ = lay["eps"]
    cst[:, o_] = EPS
    cst[:, o_ + 1] = LN_EPS
    sh = {"cst_base": cst}
    kk = np.arange(128)[:, None]
    tri = (kk <= np.arange(128)[None, :]).astype(np.float32)
    ident = np.eye(128, dtype=np.float32)
    mh = (kk <= (128 - HQ) + np.arange(HQ)[None, :]).astype(np.float32)
    NEGM = -30000.0
    qq = np.arange(512)[None, :]
    maskd = [np.where(o_ * 128 + kk <= qq, 0.0, NEGM).astype(np.float32) for o_ in range(4)]
    mhn = np.where(kk <= (128 - HQ) + np.arange(HQ)[None, :], 0.0, NEGM).astype(np.float32)
    sh["cstb"] = np.ascontiguousarray(np.concatenate(
        [tri, ident, mh, np.full((128, 128), 1.0 / D, np.float32), np.full((128, 128), 1.0 / CC, np.float32)]
        + maskd + [mhn], axis=1))
    dq = np.zeros((1, NQ), np.float32)
    for (kind, side, i, ncols, off) in group_info(cfg):
        dq[0, off:off + ncols] = (ncols - 1) - np.arange(ncols)
    sh["dq"] = dq
    w_in = inputs["w_in"][0]
    QC = NH * 128
    wq, wk, wv = w_in[:, 0:QC], w_in[:, QC:2 * QC], w_in[:, 2 * QC:3 * QC]

    def lhsT_sb(Wm):
        K, N = Wm.shape
        t = Wm.reshape(K // 128, 128, N // 128, 128).transpose(1, 2, 0, 3)
        return np.ascontiguousarray(t).reshape(128, -1)

    sh["wq"] = lhsT_sb(wq)
    sh["wk"] = lhsT_sb(wk)
    sh["wv"] = np.ascontiguousarray(wv.reshape(KD, 128, AW).transpose(1, 0, 2)).reshape(128, KD * AW)
    g0 = 3 * QC
    w3 = {}
    w3["glu"] = _lhsT_chunks(w_in[:, g0:g0 + 2 * CC])
    w3["gate"] = _lhsT_chunks(w_in[:, g0 + 2 * CC:g0 + 2 * CC + 2 * D])
    w3["abr"] = _lhsT_chunks(inputs["w_attn_br"][0])
    w3["cbr"] = _lhsT_chunks(inputs["w_conv_br"][0])
    w3["wo"] = _lhsT_chunks(inputs["w_o"][0])
    w3["up"] = _lhsT_chunks(inputs["w_up"][0])
    w3["down"] = _lhsT_chunks(inputs["w_down"][0])
    w3["pg"] = _lhsT_chunks(inputs["w_ple_gate"][0])
    w3["pp"] = _lhsT_chunks(inputs["w_ple_proj"][0])
    sh["w3"] = w3
    return sh


def run_cfg(cfg, inputs, n_cores=8, debug=False, dbg_out=None):
    inputs = {k: np.asarray(v, dtype=np.float32) for k, v in inputs.items()}
    nc = build_program(cfg, debug=debug)
    shared = prep_shared(cfg, inputs)
    in_maps = [prep_core_inputs(cfg, inputs, shared, c) for c in range(n_cores)]
    res = run_bass_kernel_spmd(nc, in_maps, core_ids=list(range(n_cores)))
    D, BLK, SEQ = cfg["D"], cfg["BLK"], cfg["SEQ"]
    if debug and dbg_out is not None:
        for c in range(n_cores):
            dbg_out.append({k: np.asarray(v) for k, v in res.results[c].items()})
    out = np.zeros((2, SEQ, D), np.float32)
    for c in range(n_cores):
        b, j = c // 4, c % 4
        o = np.asarray(res.results[c]["out"])
        out[b, j * BLK:(j + 1) * BLK] = o[:, 0:BLK].T
        out[b, (7 - j) * BLK:(8 - j) * BLK] = o[:, BLK:2 * BLK].T
    return out


def kernel(**inputs):
    cfg = make_cfg()
    return run_cfg(cfg, inputs)
```
